# Optimizing a Trainium2 kernel written in Bass

```python
import jax, jax.numpy as jnp
from jax import lax
import numpy as np

D_MODEL = 1024
BATCH = 2
SEQ = 8192
DEPTH = 2
DEC_BATCH = 128
DEC_SEQ = 4
PAST_LEN = 2048
PAGE_SIZE = 128

N_HEADS = 8
HEAD_DIM = 64
ATT_WIDTH = N_HEADS * HEAD_DIM
SG_GROUPS = 8
SG_GROUP_DIM = 64
SG_WIDTH = SG_GROUPS * SG_GROUP_DIM
CHUNK = 128
D_FF = 4 * D_MODEL
Q_BLOCK = 128
EPS = 1e-6
IN_WIDTH = 3 * ATT_WIDTH + 2 * SG_WIDTH + 2 * D_MODEL

kernel_name = "hybrid_gmlp_stickbreaking_decoder_step"


def rmsnorm(x, g):
    x32 = x.astype(jnp.float32)
    y = x32 * lax.rsqrt(jnp.mean(x32 * x32, axis=-1, keepdims=True) + EPS) * g.astype(jnp.float32)
    return y.astype(x.dtype)


def in_projection(xn, w_in, b_gate, g_sv):
    b, t = xn.shape[:2]
    h = jnp.einsum('btd,de->bte', xn, w_in)
    offs = [ATT_WIDTH, 2 * ATT_WIDTH, 3 * ATT_WIDTH, 3 * ATT_WIDTH + SG_WIDTH, 3 * ATT_WIDTH + 2 * SG_WIDTH]
    q, k, v, u, sv, gates = jnp.split(h, offs, axis=-1)
    heads = lambda a: a.reshape(b, t, N_HEADS, HEAD_DIM)
    u = jax.nn.gelu(u)
    sv = rmsnorm(jax.nn.gelu(sv), g_sv).reshape(b, t, SG_GROUPS, SG_GROUP_DIM)
    gates = jax.nn.sigmoid(gates + b_gate)
    g_att, g_sg = jnp.split(gates, 2, axis=-1)
    return heads(q), heads(k), heads(v), u, sv, g_att, g_sg


def stick_breaking_mix(q, k, v, qpos, kpos, b_sb):
    z = jnp.einsum('bqhd,bshd->bhqs', q, k).astype(jnp.float32) * (HEAD_DIM ** -0.5)
    z = z + b_sb.astype(jnp.float32)[None, :, None, None]
    mask = kpos[None, :] < qpos[:, None]
    log_1mb = jnp.where(mask, jax.nn.log_sigmoid(-z), 0.0)
    suffix = lax.cumsum(log_1mb, axis=3, reverse=True) - log_1mb
    a = jnp.where(mask, jnp.exp(jax.nn.log_sigmoid(z) + suffix), 0.0)
    return jnp.einsum('bhqs,bshd->bqhd', a.astype(v.dtype), v)


def stick_breaking_prompt(q, k, v, b_sb):
    b, s = q.shape[:2]
    nb = s // Q_BLOCK
    qb = q.reshape(b, nb, Q_BLOCK, N_HEADS, HEAD_DIM).swapaxes(0, 1)
    starts = jnp.arange(nb, dtype=jnp.int32) * Q_BLOCK
    kpos = jnp.arange(s, dtype=jnp.int32)

    def block(args):
        qblk, st = args
        return stick_breaking_mix(qblk, k, v, st + jnp.arange(Q_BLOCK, dtype=jnp.int32), kpos, b_sb)

    out = lax.map(block, (qb, starts))
    return out.swapaxes(0, 1).reshape(b, s, ATT_WIDTH)


def stick_breaking_sample(q, k_new, v_new, cache_k_l, cache_v_l, page_table, b_sb):
    db, t = q.shape[:2]
    k_past = cache_k_l[page_table].reshape(db, -1, N_HEADS, HEAD_DIM).astype(k_new.dtype)
    v_past = cache_v_l[page_table].reshape(db, -1, N_HEADS, HEAD_DIM).astype(v_new.dtype)
    p = k_past.shape[1]
    k_all = jnp.concatenate([k_past, k_new], axis=1)
    v_all = jnp.concatenate([v_past, v_new], axis=1)
    kpos = jnp.arange(p + t, dtype=jnp.int32)
    qpos = p + jnp.arange(t, dtype=jnp.int32)
    return stick_breaking_mix(q, k_all, v_all, qpos, kpos, b_sb).reshape(db, t, ATT_WIDTH)


def spatial_gating(u, sv, w_sg, b_sg):
    b, t = u.shape[:2]
    L = min(t, CHUNK)
    nc = t // L
    tril = jnp.tril(jnp.ones((CHUNK, CHUNK), dtype=bool))
    w = jnp.where(tril[None], w_sg, 0.0)[:, :L, :L]
    svc = sv.reshape(b, nc, L, SG_GROUPS, SG_GROUP_DIM)
    mixed = jnp.einsum('gts,bnsgc->bntgc', w, svc) + b_sg[:, :L].T[:, :, None]
    return u * mixed.reshape(b, t, SG_WIDTH)


def merge_branches(att, sg, g_att, g_sg, w_o_att, w_o_sg, w_o):
    merged = g_att * jnp.einsum('bte,ed->btd', att, w_o_att) + g_sg * jnp.einsum('bte,ed->btd', sg, w_o_sg)
    return jnp.einsum('btd,de->bte', merged, w_o)


def squared_relu_mlp(xn, w1, w2):
    h = jax.nn.relu(jnp.einsum('btd,df->btf', xn, w1))
    return jnp.einsum('btf,fd->btd', h * h, w2)


def setup_inputs(seed: int = 0) -> dict:
    key = jax.random.key(seed)
    ks = jax.random.split(key, 20)
    n_pages = PAST_LEN // PAGE_SIZE
    n_used = DEC_BATCH * n_pages
    n_pool = (n_used * 5) // 4
    nrm = jax.random.normal
    f32 = jnp.float32
    x_prompt = nrm(ks[0], (BATCH, SEQ, D_MODEL), f32)
    x_sample = nrm(ks[1], (DEC_BATCH, DEC_SEQ, D_MODEL), f32)
    cache_k = nrm(ks[2], (DEPTH, n_pool, PAGE_SIZE, N_HEADS, HEAD_DIM), f32)
    cache_v = nrm(ks[3], (DEPTH, n_pool, PAGE_SIZE, N_HEADS, HEAD_DIM), f32)
    page_table = jax.random.permutation(ks[4], n_pool)[:n_used].reshape(DEC_BATCH, n_pages).astype(jnp.int32)
    g_mix = 1.0 + 0.05 * nrm(ks[5], (DEPTH, D_MODEL), f32)
    w_in = nrm(ks[6], (DEPTH, D_MODEL, IN_WIDTH), f32) * D_MODEL ** -0.5
    b_gate = 0.05 * nrm(ks[7], (DEPTH, 2 * D_MODEL), f32)
    b_sb = -8.0 - 1.0 * jax.random.uniform(ks[18], (DEPTH, N_HEADS), f32)
    g_sv = 1.0 + 0.05 * nrm(ks[8], (DEPTH, SG_WIDTH), f32)
    w_sg = nrm(ks[9], (DEPTH, SG_GROUPS, CHUNK, CHUNK), f32) * CHUNK ** -0.5
    b_sg = 1.0 + 0.1 * nrm(ks[10], (DEPTH, SG_GROUPS, CHUNK), f32)
    w_o_att = nrm(ks[11], (DEPTH, ATT_WIDTH, D_MODEL), f32) * ATT_WIDTH ** -0.5
    w_o_sg = nrm(ks[12], (DEPTH, SG_WIDTH, D_MODEL), f32) * SG_WIDTH ** -0.5
    w_o = nrm(ks[13], (DEPTH, D_MODEL, D_MODEL), f32) * D_MODEL ** -0.5
    g_ffn = 1.0 + 0.05 * nrm(ks[14], (DEPTH, D_MODEL), f32)
    w_ff1 = nrm(ks[15], (DEPTH, D_MODEL, D_FF), f32) * D_MODEL ** -0.5
    w_ff2 = nrm(ks[16], (DEPTH, D_FF, D_MODEL), f32) * D_FF ** -0.5
    g_final = 1.0 + 0.05 * nrm(ks[17], (D_MODEL,), f32)
    return {"x_prompt": x_prompt, "x_sample": x_sample, "cache_k": cache_k, "cache_v": cache_v,
            "page_table": page_table, "g_mix": g_mix, "w_in": w_in, "b_gate": b_gate, "b_sb": b_sb,
            "g_sv": g_sv, "w_sg": w_sg, "b_sg": b_sg, "w_o_att": w_o_att, "w_o_sg": w_o_sg, "w_o": w_o,
            "g_ffn": g_ffn, "w_ff1": w_ff1, "w_ff2": w_ff2, "g_final": g_final}


def reference(x_prompt, x_sample, cache_k, cache_v, page_table, g_mix, w_in, b_gate, b_sb, g_sv, w_sg, b_sg,
              w_o_att, w_o_sg, w_o, g_ffn, w_ff1, w_ff2, g_final):
    xp, xs = x_prompt, x_sample
    kp_l, vp_l, ks_l, vs_l, sgv_l = [], [], [], [], []
    for l in range(DEPTH):
        q, k, v, u, sv, ga, gs = in_projection(rmsnorm(xp, g_mix[l]), w_in[l], b_gate[l], g_sv[l])
        att = stick_breaking_prompt(q, k, v, b_sb[l])
        sg = spatial_gating(u, sv, w_sg[l], b_sg[l])
        xp = xp + merge_branches(att, sg, ga, gs, w_o_att[l], w_o_sg[l], w_o[l])
        kp_l.append(k)
        vp_l.append(v)
        q, k, v, u, sv, ga, gs = in_projection(rmsnorm(xs, g_mix[l]), w_in[l], b_gate[l], g_sv[l])
        att = stick_breaking_sample(q, k, v, cache_k[l], cache_v[l], page_table, b_sb[l])
        sg = spatial_gating(u, sv, w_sg[l], b_sg[l])
        xs = xs + merge_branches(att, sg, ga, gs, w_o_att[l], w_o_sg[l], w_o[l])
        ks_l.append(k)
        vs_l.append(v)
        sgv_l.append(sv)
        xp = xp + squared_relu_mlp(rmsnorm(xp, g_ffn[l]), w_ff1[l], w_ff2[l])
        xs = xs + squared_relu_mlp(rmsnorm(xs, g_ffn[l]), w_ff1[l], w_ff2[l])
    y_prompt = rmsnorm(xp, g_final)
    y_sample = rmsnorm(xs, g_final)
    new_k_prompt = jnp.stack(kp_l)
    new_v_prompt = jnp.stack(vp_l)
    new_k_sample = jnp.stack(ks_l)
    new_v_sample = jnp.stack(vs_l)
    new_sgv_sample = jnp.stack(sgv_l)
    return (y_prompt, y_sample, new_k_prompt, new_v_prompt, new_k_sample, new_v_sample, new_sgv_sample)
```

```python
import contextlib
import numpy as np
import ml_dtypes
import concourse.bass as bass
import concourse.mybir as mybir
from concourse.bass_utils import run_bass_kernel_spmd

F32 = mybir.dt.float32
BF16 = mybir.dt.bfloat16
I32 = mybir.dt.int32
AF = mybir.ActivationFunctionType
ALU = mybir.AluOpType

D = 1024
NTG = 5
TG = 512
NTOK = NTG * TG
EPS = 1e-6
GELU_C = 0.044715
GELU_S = 1.5957691216057308
import os
NPOOL = int(os.environ.get("K_NPOOL", "2560"))
RG8 = os.environ.get("K_RG8", "0") == "1"
NOCC = os.environ.get("K_NOCC", "0") == "1"
SKIP = os.environ.get("K_SKIP", "")


def groups_of(j):
    return [j, 7 - j, 8 + j, 15 - j]


def owner_of(G):
    for j in range(4):
        gs = groups_of(j)
        if G in gs:
            return j, gs.index(G)
    raise ValueError


class Prog:
    ENGS = ("pe", "act", "dve", "pool", "sp")

    def __init__(self, nc, stack):
        self.nc = nc
        self.stack = stack
        self.ops = {e: [] for e in self.ENGS}
        self.cnt = {e: 0 for e in self.ENGS}
        self.esem = {e: stack.enter_context(nc.semaphore("es_" + e)) for e in self.ENGS}
        self.lastw = {}
        self.reads = {}
        self.bsem = {}
        self.bcnt = {}
        self.waited = {e: {} for e in self.ENGS}
        self.sems = {}
        self.alltok = {}
        for e in self.ENGS:
            self.sems[id(self.esem[e])] = self.esem[e]

    def _tok_sem(self, sem):
        self.sems[id(sem)] = sem
        return id(sem)

    def _deps(self, eng, reads, writes):
        toks = {}

        def add(t, skip_same=False):
            if t is None:
                return
            sid, val, teng = t
            if teng == "pe" and eng == "pe":
                return
            if skip_same and teng == eng:
                return
            if toks.get(sid, 0) < val:
                toks[sid] = val
        for b in list(reads) + list(writes):
            add(self.lastw.get(b))
        for b in reads:
            if b.startswith("ps"):
                for t in self.reads.get(b, ()):
                    add(t, skip_same=True)
        for b in writes:
            for t in self.reads.get(b, ()):
                add(t)
        out = []
        for sid, val in toks.items():
            if self.waited[eng].get(sid, 0) >= val:
                continue
            self.waited[eng][sid] = val
            out.append((sid, val))
        return out

    def _commit(self, tok, reads, writes):
        for b in writes:
            self.lastw[b] = tok
            self.reads[b] = []
        for b in reads:
            self.reads.setdefault(b, []).append(tok)
        sid, val, _ = tok
        if self.alltok.get(sid, 0) < val:
            self.alltok[sid] = val

    def op(self, eng, fn, reads=(), writes=()):
        waits = self._deps(eng, reads, writes)
        self.cnt[eng] += 1
        tok = (id(self.esem[eng]), self.cnt[eng], eng)
        self.ops[eng].append((waits, fn, self.esem[eng], 1))
        self._commit(tok, reads, writes)

    def dma(self, eng, fn, reads=(), writes=(), key=None, inc=16):
        waits = self._deps(eng, reads, writes)
        if key not in self.bsem:
            self.bsem[key] = self.stack.enter_context(self.nc.semaphore("bs%d" % len(self.bsem)))
            self.bcnt[key] = 0
            self._tok_sem(self.bsem[key])
        self.bcnt[key] += inc
        tok = (id(self.bsem[key]), self.bcnt[key], "dma")
        self.ops[eng].append((waits, fn, self.bsem[key], inc))
        self._commit(tok, reads, writes)

    def wait_all(self, e):
        waits = []
        for sid, val in self.alltok.items():
            if self.waited[e].get(sid, 0) >= val or sid == id(self.esem[e]):
                continue
            self.waited[e][sid] = val
            waits.append((sid, val))
        if waits:
            self.ops[e].append((waits, None, None, 0))

    def wait_buf(self, e, key):
        t = self.lastw.get(key)
        if t is None:
            return
        sid, val, _ = t
        if self.waited[e].get(sid, 0) >= val:
            return
        self.waited[e][sid] = val
        self.ops[e].append(([(sid, val)], None, None, 0))

    def barrier(self):
        for e in self.ENGS:
            waits = []
            for sid, val in self.alltok.items():
                if self.waited[e].get(sid, 0) >= val:
                    continue
                if sid == id(self.esem[e]):
                    continue
                self.waited[e][sid] = val
                waits.append((sid, val))
            if waits:
                self.ops[e].append((waits, None, None, 0))

    def flush(self):
        ops = self.ops
        self.ops = {e: [] for e in self.ENGS}
        sems = self.sems

        def run(h, lst):
            for waits, fn, sem, inc in lst:
                for sid, val in waits:
                    h.wait_ge(sems[sid], val)
                if fn is not None:
                    ins = fn(h)
                    ins.then_inc(sem, inc)
        with self.nc.Block() as blk:
            @blk.tensor
            def _(h):
                run(h, ops["pe"])

            @blk.scalar
            def _(h):
                run(h, ops["act"])

            @blk.vector
            def _(h):
                run(h, ops["dve"])

            @blk.gpsimd
            def _(h):
                run(h, ops["pool"])

            @blk.sync
            def _(h):
                run(h, ops["sp"])


class _Stop(Exception):
    pass


def build_nc(stop=None):
    nc = bass.Bass("TRN2", target_bir_lowering=False)

    def chk(tag):
        if stop is not None and stop == tag:
            raise _Stop()

    def stop_here(tag, P):
        if stop is not None and stop == tag:
            P.barrier()
            P.flush()
            raise _Stop()

    def din(name, shape, dt=F32):
        return nc.dram_tensor(name, list(shape), dt, kind="ExternalInput").ap()

    def dout(name, shape, dt=F32):
        return nc.dram_tensor(name, list(shape), dt, kind="ExternalOutput").ap()

    xp = din("xp", [D, 2048])
    xs = din("xs", [D, 512])
    win = din("win", [2, 128, 8, 4608])
    wsamp = din("wsamp", [2, 128, 8, 192])
    woatt = din("woatt", [2, 64, 8, 1024])
    wosg = din("wosg", [2, 64, 8, 1024])
    wo = din("wo", [2, 128, 8, 1024])
    w1 = din("w1", [2, 128, 8, 4096])
    w2 = din("w2", [2, 128, 32, 1024])
    gmix = din("gmix", [2, 128, 8])
    gffn = din("gffn", [2, 128, 8])
    gfin = din("gfin", [128, 8])
    bgate = din("bgate", [2, 128, 16])
    bsb = din("bsb", [2, 128, 8])
    bsbc = din("bsbc", [2, 128, 1])
    gsv = din("gsv", [2, 128, 512])
    wsgT = din("wsgT", [2, 128, 8, 128])
    bsg = din("bsg", [2, 1, 1024])
    wsg4T = din("wsg4T", [2, 128, 8, 128])
    bsg4 = din("bsg4", [2, 1, 1024])
    cst = din("cst", [128, 6 * 128])
    masks = din("masks", [64, 128, 512], BF16)
    poolK = [din("poolK%d" % l, [NPOOL, 8192]) for l in range(2)]
    poolV = [din("poolV%d" % l, [NPOOL, 8192]) for l in range(2)]
    ptab = din("ptab", [16, 128, 1], I32)

    yp = dout("yp", [D, 2048])
    ys = dout("ys", [D, 512])
    nkp = dout("nkp", [2, 512, 2048])
    nvp = dout("nvp", [2, 2048, 512])
    nks = dout("nks", [2, 64, 512])
    nvs = dout("nvs", [2, 512, 64])
    sgv = dout("sgv", [2, 512, 512])

    ag_in = [[nc.dram_tensor("ag_in%d_%d" % (l, t), [1024, 512], BF16) for t in range(4)] for l in range(2)]
    ag_out = [[nc.dram_tensor("ag_out%d_%d" % (l, t), [4 * 1024, 512], BF16) for t in range(4)] for l in range(2)]
    ag2_in = [nc.dram_tensor("ag2_in%d" % l, [64, 512], BF16) for l in range(2)]
    ag2_out = [nc.dram_tensor("ag2_out%d" % l, [512, 512], BF16) for l in range(2)]
    Kg = nc.dram_tensor("Kg", [2048, 8192], F32).ap()
    Vg = nc.dram_tensor("Vg", [2048, 8192], F32).ap()

    stack = contextlib.ExitStack()
    try:
     with stack:
        P = Prog(nc, stack)

        uniq = {"n": 0}

        def sb(name, shape, dt, st=stack):
            uniq["n"] += 1
            return st.enter_context(nc.sbuf_tensor("%s_%d" % (name, uniq["n"]), list(shape), dt))

        psA = stack.enter_context(nc.psum_tensor("psA", [128, 1536], F32))
        psB = stack.enter_context(nc.psum_tensor("psB", [128, 1536], F32))
        ps6 = stack.enter_context(nc.psum_tensor("ps6", [128, 512], F32))
        ps7 = stack.enter_context(nc.psum_tensor("ps7", [128, 512], F32))

        def bank(i):
            if i < 3:
                return psA[:, i * 512:(i + 1) * 512]
            if i < 6:
                return psB[:, (i - 3) * 512:(i - 2) * 512]
            return ps6[:, :] if i == 6 else ps7[:, :]

        def bk(i):
            return "ps%d" % i

        xT = sb("xT", [128, 8, NTOK], F32)
        cst_f = sb("cst_f", [128, 6 * 128], F32)
        tri_bf = sb("tri_bf", [128, 2 * 128], BF16)
        ones_bf = sb("ones_bf", [128, 128], BF16)
        nones_bf = sb("nones_bf", [128, 128], BF16)
        gmix_sb = sb("gmix_sb", [128, 2, 8], F32)
        gffn_sb = sb("gffn_sb", [128, 2, 8], F32)
        gfin_sb = sb("gfin_sb", [128, 8], F32)
        bgate_sb = sb("bgate_sb", [128, 2, 16], F32)
        bsb_sb = sb("bsb_sb", [128, 2, 8], F32)
        bsbc_sb = sb("bsbc_sb", [128, 2, 1], F32)
        ptab_sb = [sb("ptab_sb%d" % i, [128, 1], I32) for i in range(16)]
        rstd = sb("rstd", [128, 512], F32)
        xn = sb("xn", [128, 8, 512], BF16)
        sq = xn
        lnv = rstd
        wring = [sb("wring%d" % i, [128, 4096], BF16) for i in range(3)]
        wstate = {"n": 0}

        MGT = cst_f[:, 256:384]
        MLE = cst_f[:, 384:512]
        MB = cst_f[0:64, 512:576]
        MBLE = cst_f[:, 640:768]

        def cols(tg):
            return slice(tg * TG, (tg + 1) * TG)

        def load_w(src_ap, parts, shape):
            i = wstate["n"] % 3
            wstate["n"] += 1
            key = "wring%d" % i
            n = int(np.prod(shape))
            view = wring[i][0:parts, 0:n]
            if len(shape) == 2:
                view = view.rearrange("p (a b) -> p a b", a=shape[0])
            P.dma("pool", lambda h, v=view, s=src_ap: h.dma_start(out=v, in_=s), reads=(), writes=(key,), key=key)
            return view, key

        def rmsnorm(tg, gt, extra_reads=()):
            c = cols(tg)
            P.op("act", lambda h: h.activation(out=sq[:, :, :], in_=xT[:, :, c], func=AF.Square),
                 reads=("xT%d" % tg,), writes=("xn",))
            for kc in range(8):
                P.op("pe", lambda h, kc=kc: h.matmul(bank(7), lhsT=ones_bf[:, :], rhs=sq[:, kc, :], start=(kc == 0), stop=(kc == 7)),
                     reads=("xn", "ones"), writes=(bk(7),))
            P.op("act", lambda h: h.activation(out=lnv[:, :], in_=bank(7), func=AF.Ln, bias=EPS, scale=1.0 / D),
                 reads=(bk(7),), writes=("rstd",))
            P.op("act", lambda h: h.activation(out=rstd[:, :], in_=lnv[:, :], func=AF.Exp, scale=-0.5),
                 reads=("rstd",), writes=("rstd",))
            for kc in range(8):
                P.op("dve", lambda h, kc=kc: h.scalar_tensor_tensor(out=xn[:, kc, :], in0=xT[:, kc, c], scalar=gt[:, kc:kc + 1],
                                                                      in1=rstd[:, :], op0=ALU.mult, op1=ALU.mult),
                     reads=("xT%d" % tg, "rstd", "params"), writes=("xn",))

        for tg in range(4):
            P.dma("sp", lambda h, tg=tg: h.dma_start(out=xT[:, :, cols(tg)], in_=xp[:, cols(tg)].rearrange("(kc kp) t -> kp kc t", kp=128)),
                  writes=("xT%d" % tg,), key="xT%d" % tg)
        P.dma("sp", lambda h: h.dma_start(out=xT[:, :, cols(4)], in_=xs[:, :].rearrange("(kc kp) t -> kp kc t", kp=128)),
              writes=("xT4",), key="xT4")
        P.dma("sp", lambda h: h.dma_start(out=cst_f[:, :], in_=cst[:, :]), writes=("cst",), key="cst")
        P.dma("pool", lambda h: h.dma_start(out=tri_bf[:, :], in_=cst[:, 0:256]), writes=("tri",), key="tri")
        for (t_sb, t_dr, nm) in ((gmix_sb, gmix, "p0"), (gffn_sb, gffn, "p1"), (bgate_sb, bgate, "p2"), (bsb_sb, bsb, "p3"), (bsbc_sb, bsbc, "p4")):
            P.dma("sp", lambda h, t_sb=t_sb, t_dr=t_dr: h.dma_start(out=t_sb[:, :, :], in_=t_dr.rearrange("l p k -> p l k"), allow_slow_non_contiguous=True),
                  writes=("params",), key=nm)
        P.dma("sp", lambda h: h.dma_start(out=gfin_sb[:, :], in_=gfin[:, :]), writes=("params",), key="p5")
        for i in range(16):
            P.dma("sp", lambda h, i=i: h.dma_start(out=ptab_sb[i][:, :], in_=ptab[i, :, :]), writes=("ptab",), key="p6_%d" % i)
        P.op("dve", lambda h: h.memset(ones_bf[:, :], 1.0), writes=("ones",))
        P.op("dve", lambda h: h.memset(nones_bf[:, :], -1.0), writes=("nones",))
        P.barrier()
        P.flush()
        chk("init")

        NEGTRI = tri_bf[:, 0:128]
        NEGTRIC = tri_bf[:, 128:256]

        for l in range(2):
            lstack = contextlib.ExitStack()
            with lstack:
                qsT = sb("qsT", [64, 512], F32, lstack)
                ksT = sb("ksT", [64, 512], F32, lstack)
                vsn = sb("vsn", [64, 8, 64], F32, lstack)
                agi = [t.ap() for t in ag_in[l]]
                ago = [t.ap() for t in ag_out[l]]

                astack = contextlib.ExitStack()
                with astack:
                    kst = sb("kst", [128, 4, 512], F32, astack)
                    kbf = sb("kbf", [128, 4, 512], BF16, astack)
                    vst = sb("vst", [128, 4, 512], F32, astack)
                    vbf = sb("vbf", [128, 4, 512], BF16, astack)
                    for tg in range(4):
                        rmsnorm(tg, gmix_sb[:, l, :])
                        stop_here("A%dr%d" % (l, tg), P)
                        wk, wk_key = load_w(win[l, :, :, 512:1024], 128, (8, 512))
                        wv, wv_key = load_w(win[l, :, :, 1024:1536], 128, (8, 512))
                        for e in range(4):
                            b = e % 2
                            for kc in range(8):
                                P.op("pe", lambda h, b=b, e=e, kc=kc, wk=wk: h.matmul(bank(b), lhsT=wk[:, kc, e * 128:(e + 1) * 128], rhs=xn[:, kc, :],
                                                                                    start=(kc == 0), stop=(kc == 7)),
                                     reads=("xn", wk_key), writes=(bk(b),))
                            if "c" not in SKIP:
                                P.op("act", lambda h, b=b, e=e: h.activation(out=kst[:, e, :], in_=bank(b), func=AF.Copy),
                                     reads=(bk(b),), writes=("kst",))
                            if "d" not in SKIP:
                                P.op("dve", lambda h, b=b, e=e: h.tensor_copy(out=kbf[:, e, :], in_=bank(b)),
                                     reads=(bk(b),), writes=("kbf",))
                        if "a" not in SKIP:
                            P.dma("sp", lambda h, tg=tg: h.dma_start(out=nkp[l, :, cols(tg)].rearrange("(e p) t -> p e t", p=128), in_=kst[:, :, :]),
                                  reads=("kst",), key="kst")
                        if "b" not in SKIP:
                            P.dma("sp", lambda h, tg=tg: h.dma_start(out=agi[tg][0:512, :].rearrange("(e p) t -> p e t", p=128), in_=kbf[:, :, :]),
                                  reads=("kbf",), writes=("agi%d" % tg,), key="kbf")
                        stop_here("A%dk%d" % (l, tg), P)
                        for tb in range(4):
                            b = 2 + tb % 2
                            for kc in range(8):
                                P.op("pe", lambda h, b=b, tb=tb, kc=kc, wv=wv: h.matmul(bank(b), lhsT=xn[:, kc, tb * 128:(tb + 1) * 128], rhs=wv[:, kc, :],
                                                                                      start=(kc == 0), stop=(kc == 7)),
                                     reads=("xn", wv_key), writes=(bk(b),))
                            P.op("act", lambda h, b=b, tb=tb: h.activation(out=vst[:, tb, :], in_=bank(b), func=AF.Copy),
                                 reads=(bk(b),), writes=("vst",))
                            P.op("dve", lambda h, b=b, tb=tb: h.tensor_copy(out=vbf[:, tb, :], in_=bank(b)),
                                 reads=(bk(b),), writes=("vbf",))
                        P.dma("sp", lambda h, tg=tg: h.dma_start(out=nvp[l, cols(tg), :].rearrange("(b p) e -> p b e", p=128), in_=vst[:, :, :]),
                              reads=("vst",), key="vst")
                        P.dma("sp", lambda h, tg=tg: h.dma_start(out=agi[tg][512:1024, :].rearrange("(b p) e -> p b e", p=128), in_=vbf[:, :, :]),
                              reads=("vbf",), writes=("agi%d" % tg,), key="vbf")
                        if not NOCC:
                            P.wait_all("pool")
                            P.dma("pool", lambda h, tg=tg: h.collective_compute("AllGather", ALU.bypass, replica_groups=[[0, 1, 2, 3], [4, 5, 6, 7]],
                                                                                ins=[ag_in[l][tg].ap().opt()], outs=[ag_out[l][tg].ap().opt()]),
                                  reads=("agi%d" % tg,), writes=("ago%d" % tg,), key="cc1_%d_%d" % (l, tg), inc=1)
                            P.wait_buf("pool", "ago%d" % tg)
                        stop_here("A%dv%d" % (l, tg), P)
                    stop_here("A%dp" % l, P)
                    rmsnorm(4, gmix_sb[:, l, :])
                    ws, ws_key = load_w(wsamp[l, :, :, :], 128, (8, 192))
                    for (j, dst, scl, nm) in ((0, qsT, 0.125, "qsT"), (1, ksT, 1.0, "ksT")):
                        for kc in range(8):
                            P.op("pe", lambda h, j=j, kc=kc, ws=ws: h.matmul(bank(j)[0:64, :], lhsT=ws[:, kc, j * 64:(j + 1) * 64], rhs=xn[:, kc, :],
                                                                           start=(kc == 0), stop=(kc == 7)),
                                 reads=("xn", ws_key), writes=(bk(j),))
                        P.op("act", lambda h, j=j, dst=dst, scl=scl: h.activation(out=dst[:, :], in_=bank(j)[0:64, :], func=AF.Copy, scale=scl),
                             reads=(bk(j),), writes=(nm,))
                    P.dma("sp", lambda h: h.dma_start(out=nks[l, :, :], in_=ksT[:, :]), reads=("ksT",), key="ksTo")
                    for bt in range(8):
                        for kc in range(8):
                            P.op("pe", lambda h, bt=bt, kc=kc, ws=ws: h.matmul(bank(2)[0:64, bt * 64:(bt + 1) * 64], lhsT=xn[:, kc, bt * 64:(bt + 1) * 64],
                                                                             rhs=ws[:, kc, 128:192], start=(kc == 0), stop=(kc == 7)),
                                 reads=("xn", ws_key), writes=(bk(2),))
                    P.op("act", lambda h: h.activation(out=vsn[:, :, :], in_=bank(2)[0:64, :].rearrange("p (b d) -> p b d", d=64), func=AF.Copy),
                         reads=(bk(2),), writes=("vsn",))
                    P.dma("sp", lambda h: h.dma_start(out=nvs[l, :, :].rearrange("(b p) d -> p b d", p=64), in_=vsn[:, :, :]), reads=("vsn",), key="vsno")
                    P.barrier()
                    P.flush()
                chk("A%d" % l)

                sstack = contextlib.ExitStack()
                with sstack:
                    gstack = contextlib.ExitStack()
                    gts = [sb("gt", [128, 8192], F32, gstack) for _ in range(2)]
                    gi = 0
                    for (pool_d, scr, nm) in ((poolK, Kg, "Kg"), (poolV, Vg, "Vg")):
                        for gg in range(16):
                            gtb = gts[gi % 2]
                            gk = "gt%d" % (gi % 2)
                            gi += 1
                            P.dma("pool", lambda h, gg=gg, pool_d=pool_d, gtb=gtb: h.indirect_dma_start(
                                out=gtb[:, :], out_offset=None, in_=pool_d[l][:, :],
                                in_offset=bass.IndirectOffsetOnAxis(ap=ptab_sb[gg][:, 0:1], axis=0)),
                                reads=("ptab",), writes=(gk,), key=gk)
                            P.dma("sp", lambda h, gg=gg, scr=scr, gtb=gtb: h.dma_start(out=scr[gg * 128:(gg + 1) * 128, :], in_=gtb[:, :]),
                                  reads=(gk,), writes=(nm + str(gg),), key=gk)
                    P.barrier()
                    P.flush()
                    gstack.close()
                    chk("G%d" % l)
                    KTs = [sb("KTs%d" % i, [64, 16, 128], F32, sstack) for i in range(2)]
                    Vs = [sb("Vs%d" % i, [128, 16, 64], F32, sstack) for i in range(2)]
                    e_s = sb("e_s", [128, 17, 64], F32, sstack)
                    f_s = sb("f_s", [128, 17, 64], BF16, sstack)
                    E_s = sb("E_s", [128, 17, 64], F32, sstack)
                    A_s = sb("A_s", [128, 17, 64], F32, sstack)
                    attS = sb("attS", [64, 512], BF16, sstack)
                    zps = psA[:, 0:1088].rearrange("p (a b) -> p a b", b=64)
                    cps = psB[:, 0:1088].rearrange("p (a b) -> p a b", b=64)
                    bsc = bsbc_sb[:, l, :]
                    for bt in range(8):
                        for s in range(16):
                            seq = bt * 16 + s
                            sl = seq % 2
                            P.dma("sp", lambda h, seq=seq, sl=sl: h.dma_start(out=KTs[sl][:, :, :], in_=Kg[seq * 16:(seq + 1) * 16, :].rearrange("p (d t) -> d p t", d=64)),
                                  reads=("Kg%d" % (seq // 8),), writes=("KTs%d" % sl,), key="KTs%d" % sl)
                            for p in range(16):
                                P.op("pe", lambda h, p=p, s=s, seq=seq, sl=sl: h.matmul(zps[:, p, s * 4:(s + 1) * 4], lhsT=KTs[sl][:, p, :], rhs=qsT[:, seq * 4:(seq + 1) * 4],
                                                                                      start=True, stop=True),
                                     reads=("KTs%d" % sl, "qsT"), writes=("psA",))
                        P.op("pe", lambda h, bt=bt: h.matmul(zps[0:64, 16, :], lhsT=ksT[:, bt * 64:(bt + 1) * 64], rhs=qsT[:, bt * 64:(bt + 1) * 64], start=True, stop=True),
                             reads=("ksT", "qsT"), writes=("psA",))
                        for hb in range(2):
                            P.op("act", lambda h, hb=hb: h.activation(out=e_s[:, hb * 8:(hb + 1) * 8, :], in_=zps[:, hb * 8:(hb + 1) * 8, :], func=AF.Exp, bias=bsc),
                                 reads=("psA", "params"), writes=("e_s",))
                        P.op("act", lambda h: h.activation(out=e_s[0:64, 16, :], in_=zps[0:64, 16, :], func=AF.Exp, bias=bsc[0:64, :]),
                             reads=("psA", "params"), writes=("e_s",))
                        P.op("dve", lambda h: h.tensor_tensor(out=e_s[0:64, 16, :], in0=e_s[0:64, 16, :], in1=MB, op=ALU.mult),
                             reads=("e_s", "cst"), writes=("e_s",))
                        P.op("act", lambda h: h.activation(out=f_s[:, 0:16, :], in_=e_s[:, 0:16, :], func=AF.Ln, bias=1.0),
                             reads=("e_s",), writes=("f_s",))
                        P.op("act", lambda h: h.activation(out=f_s[0:64, 16, :], in_=e_s[0:64, 16, :], func=AF.Ln, bias=1.0),
                             reads=("e_s",), writes=("f_s",))
                        for p in range(16):
                            P.op("pe", lambda h, p=p: h.matmul(cps[:, p, :], lhsT=NEGTRI, rhs=f_s[:, p, :], start=True, stop=False),
                                 reads=("f_s", "tri"), writes=("psB",))
                            for p2 in range(p + 1, 16):
                                P.op("pe", lambda h, p=p, p2=p2: h.matmul(cps[:, p, :], lhsT=nones_bf[:, :], rhs=f_s[:, p2, :], start=False, stop=False),
                                     reads=("f_s", "nones"), writes=("psB",))
                            P.op("pe", lambda h, p=p: h.matmul(cps[:, p, :], lhsT=nones_bf[0:64, :], rhs=f_s[0:64, 16, :], start=False, stop=True),
                                 reads=("f_s", "nones"), writes=("psB",))
                        P.op("pe", lambda h: h.matmul(cps[0:64, 16, :], lhsT=NEGTRI[0:64, 0:64], rhs=f_s[0:64, 16, :], start=True, stop=True),
                             reads=("f_s", "tri"), writes=("psB",))
                        for hb in range(2):
                            P.op("act", lambda h, hb=hb: h.activation(out=E_s[:, hb * 8:(hb + 1) * 8, :], in_=cps[:, hb * 8:(hb + 1) * 8, :], func=AF.Exp),
                                 reads=("psB",), writes=("E_s",))
                        P.op("act", lambda h: h.activation(out=E_s[0:64, 16, :], in_=cps[0:64, 16, :], func=AF.Exp),
                             reads=("psB",), writes=("E_s",))
                        P.op("dve", lambda h: h.tensor_tensor(out=A_s[:, 0:16, :], in0=e_s[:, 0:16, :], in1=E_s[:, 0:16, :], op=ALU.mult),
                             reads=("e_s", "E_s"), writes=("A_s",))
                        P.op("dve", lambda h: h.tensor_tensor(out=A_s[0:64, 16, :], in0=e_s[0:64, 16, :], in1=E_s[0:64, 16, :], op=ALU.mult),
                             reads=("e_s", "E_s"), writes=("A_s",))
                        att = bank(6)[0:64, 0:64]
                        P.op("pe", lambda h, bt=bt: h.matmul(att, lhsT=vsn[:, bt, :], rhs=A_s[0:64, 16, :], start=True, stop=False),
                             reads=("vsn", "A_s"), writes=(bk(6),))
                        for s in range(16):
                            seq = bt * 16 + s
                            sl = seq % 2
                            P.dma("sp", lambda h, seq=seq, sl=sl: h.dma_start(out=Vs[sl][:, :, :], in_=Vg[seq * 16:(seq + 1) * 16, :].rearrange("p (t d) -> t p d", t=128)),
                                  reads=("Vg%d" % (seq // 8),), writes=("Vs%d" % sl,), key="Vs%d" % sl)
                            for p in range(16):
                                P.op("pe", lambda h, p=p, s=s, sl=sl: h.matmul(bank(6)[0:64, s * 4:(s + 1) * 4], lhsT=Vs[sl][:, p, :], rhs=A_s[:, p, s * 4:(s + 1) * 4],
                                                                             start=False, stop=(p == 15)),
                                     reads=("Vs%d" % sl, "A_s"), writes=(bk(6),))
                        P.op("act", lambda h, bt=bt: h.activation(out=attS[:, bt * 64:(bt + 1) * 64], in_=att, func=AF.Copy),
                             reads=(bk(6),), writes=("attS",))
                    P.dma("sp", lambda h: h.dma_start(out=ag2_in[l].ap()[:, :], in_=attS[:, :]), reads=("attS",), writes=("ag2i",), key="attS")
                    if not NOCC:
                        P.wait_all("pool")
                        P.dma("pool", lambda h: h.collective_compute("AllGather", ALU.bypass, replica_groups=[list(range(8))],
                                                                     ins=[ag2_in[l].ap().opt()], outs=[ag2_out[l].ap().opt()]),
                              reads=("ag2i",), writes=("ag2o",), key="cc2_%d" % l, inc=1)
                        P.wait_buf("pool", "ag2o")
                    P.barrier()
                    P.flush()
                chk("S%d" % l)

                for tg in range(NTG):
                    bstack = contextlib.ExitStack()
                    with bstack:
                        attT = sb("attT", [64, 8, 512], BF16, bstack)
                        if tg < 4:
                            atstack = contextlib.ExitStack()
                            QT = sb("QT", [128, 4, 512], BF16, atstack)
                            KT = sb("KT", [128, 2048 * 4], BF16, atstack)
                            Vt = sb("Vt", [128, 64, 128], BF16, atstack)
                            NE, NF, NM = 6, 4, 3
                            e_r = [sb("e_t", [128, 512], F32, atstack) for _ in range(NE)]
                            f_r = [sb("f_t", [128, 512], BF16, atstack) for _ in range(NF)]
                            E_r = [sb("E_t", [128, 512], F32, atstack) for _ in range(2)]
                            A_r = [sb("A_t", [128, 512], BF16, atstack) for _ in range(2)]
                            mk_r = [sb("mk", [128, 512], BF16, atstack) for _ in range(NM)]
                            nkb = 16 * (tg + 1)
                            if os.environ.get("K_DEBUG"):
                                print("attn scope sbuf remaining", nc.sbuf_bytes_remaining)
                            rmsnorm(tg, gmix_sb[:, l, :])
                            wq, wq_key = load_w(win[l, :, :, 0:512], 128, (8, 512))
                            for e in range(4):
                                b = e % 2
                                for kc in range(8):
                                    P.op("pe", lambda h, b=b, e=e, kc=kc, wq=wq: h.matmul(bank(b), lhsT=wq[:, kc, e * 128:(e + 1) * 128], rhs=xn[:, kc, :],
                                                                                        start=(kc == 0), stop=(kc == 7)),
                                         reads=("xn", wq_key), writes=(bk(b),))
                                P.op("act", lambda h, b=b, e=e: h.activation(out=QT[:, e, :], in_=bank(b), func=AF.Copy, scale=0.125),
                                     reads=(bk(b),), writes=("QT",))
                            for hp in range(4):
                                for G in range(4 * (tg + 1)):
                                    j, li = owner_of(G)
                                    P.dma("sp", lambda h, G=G, j=j, li=li, hp=hp: h.dma_start(
                                        out=KT[:, G * 512:(G + 1) * 512], in_=ago[li][j * 1024 + hp * 128: j * 1024 + (hp + 1) * 128, :]),
                                        reads=("ago%d" % li,), writes=("KT",), key="KT")
                                    P.dma("sp", lambda h, G=G, j=j, li=li, hp=hp: h.dma_start(
                                        out=Vt[:, G * 4:(G + 1) * 4, :],
                                        in_=ago[li][j * 1024 + 512: j * 1024 + 1024, hp * 128:(hp + 1) * 128].rearrange("(b p) c -> p b c", p=128)),
                                        reads=("ago%d" % li,), writes=("Vt",), key="Vt")
                                inst = [(hh, u) for u in range(nkb) for hh in range(2)]
                                ninst = len(inst)
                                ZB = (0, 1)
                                SB_ = (2, 3)
                                OB = (4, 5)

                                def ok(i):
                                    return 0 <= i < ninst

                                def st_z(i):
                                    hh, u = inst[i]
                                    kb = nkb - 1 - u
                                    pr = slice(hh * 64, (hh + 1) * 64)
                                    zb = ZB[i % 2]
                                    P.op("pe", lambda h, kb=kb, pr=pr, zb=zb, hp=hp: h.matmul(bank(zb), lhsT=KT[pr, kb * 128:(kb + 1) * 128], rhs=QT[pr, hp, :], start=True, stop=True),
                                         reads=("KT", "QT"), writes=(bk(zb),))
                                    if u < 16 and hh == 0:
                                        mb_ = u % NM
                                        midx = tg * 16 + (kb - 16 * tg)
                                        P.dma("sp", lambda h, midx=midx, mb_=mb_: h.dma_start(out=mk_r[mb_][:, :], in_=masks[midx, :, :]), writes=("mk%d" % mb_,), key="mk%d" % mb_)

                                def st_e(i):
                                    hh, u = inst[i]
                                    zb = ZB[i % 2]
                                    eb = i % NE
                                    bias_h = bsb_sb[:, l, hp * 2 + hh:hp * 2 + hh + 1]
                                    P.op("act", lambda h, zb=zb, eb=eb, bias_h=bias_h: h.activation(out=e_r[eb][:, :], in_=bank(zb), func=AF.Exp, bias=bias_h),
                                         reads=(bk(zb), "params"), writes=("e%d" % eb,))

                                def st_mask(i):
                                    hh, u = inst[i]
                                    eb = i % NE
                                    if u < 16:
                                        mb_ = u % NM
                                        P.op("dve", lambda h, eb=eb, mb_=mb_: h.tensor_tensor(out=e_r[eb][:, :], in0=e_r[eb][:, :], in1=mk_r[mb_][:, :], op=ALU.mult),
                                             reads=("e%d" % eb, "mk%d" % mb_), writes=("e%d" % eb,))

                                def st_f(i):
                                    eb = i % NE
                                    fb = i % NF
                                    P.op("act", lambda h, eb=eb, fb=fb: h.activation(out=f_r[fb][:, :], in_=e_r[eb][:, :], func=AF.Ln, bias=1.0),
                                         reads=("e%d" % eb,), writes=("f%d" % fb,))

                                def st_mm2(i):
                                    hh, u = inst[i]
                                    fb = i % NF
                                    sbk = SB_[hh]
                                    P.op("pe", lambda h, u=u, fb=fb, sbk=sbk: h.matmul(bank(sbk), lhsT=NEGTRI, rhs=f_r[fb][:, :], start=(u == 0), stop=False),
                                         reads=("f%d" % fb, "tri"), writes=(bk(sbk),))

                                def st_E(i):
                                    hh, u = inst[i]
                                    sbk = SB_[hh]
                                    P.op("act", lambda h, i=i, sbk=sbk: h.activation(out=E_r[i % 2][:, :], in_=bank(sbk), func=AF.Exp),
                                         reads=(bk(sbk),), writes=("E%d" % (i % 2),))

                                def st_mm3(i):
                                    hh, u = inst[i]
                                    fb = i % NF
                                    sbk = SB_[hh]
                                    P.op("pe", lambda h, u=u, fb=fb, sbk=sbk: h.matmul(bank(sbk), lhsT=NEGTRIC, rhs=f_r[fb][:, :], start=False, stop=(u == nkb - 1)),
                                         reads=("f%d" % fb, "tri"), writes=(bk(sbk),))

                                def st_A(i):
                                    eb = i % NE
                                    P.op("dve", lambda h, i=i, eb=eb: h.tensor_tensor(out=A_r[i % 2][:, :], in0=e_r[eb][:, :], in1=E_r[i % 2][:, :], op=ALU.mult),
                                         reads=("e%d" % eb, "E%d" % (i % 2)), writes=("A%d" % (i % 2),))

                                def st_av(i):
                                    hh, u = inst[i]
                                    kb = nkb - 1 - u
                                    obk = OB[hh]
                                    P.op("pe", lambda h, i=i, u=u, kb=kb, hh=hh, obk=obk: h.matmul(bank(obk)[0:64, :], lhsT=Vt[:, kb, hh * 64:(hh + 1) * 64], rhs=A_r[i % 2][:, :],
                                                                                                  start=(u == 0), stop=(u == nkb - 1)),
                                         reads=("Vt", "A%d" % (i % 2)), writes=(bk(obk),))

                                for sstep in range(ninst + 6):
                                    if ok(sstep - 5):
                                        st_mm3(sstep - 5)
                                    if ok(sstep - 3):
                                        st_mm2(sstep - 3)
                                    if ok(sstep - 4):
                                        st_E(sstep - 4)
                                    if ok(sstep - 5):
                                        st_A(sstep - 5)
                                    if ok(sstep - 6):
                                        st_av(sstep - 6)
                                    if ok(sstep):
                                        st_z(sstep)
                                    if ok(sstep - 1):
                                        st_e(sstep - 1)
                                        st_mask(sstep - 1)
                                    if ok(sstep - 2):
                                        st_f(sstep - 2)
                                for hh in range(2):
                                    hd = hp * 2 + hh
                                    P.op("dve", lambda h, hd=hd, hh=hh: h.tensor_copy(out=attT[:, hd, :], in_=bank(OB[hh])[0:64, :]),
                                         reads=(bk(OB[hh]),), writes=("attT",))
                            P.barrier()
                            P.flush()
                            atstack.close()
                            chk("T%d_%d" % (l, tg))
                        else:
                            P.dma("sp", lambda h: h.dma_start(out=attT[:, :, :], in_=ag2_out[l].ap().rearrange("(hh d) t -> d hh t", d=64)),
                                  reads=("ag2o",), writes=("attT",), key="attT")

                        uT = sb("uT", [64, 8, 512], BF16, bstack)
                        t1 = sb("t1", [128, 512], F32, bstack)
                        t2 = sb("t2", [128, 512], F32, bstack)
                        svg = sb("svg", [128, 512], F32, bstack)
                        svn = sb("svn", [128, 512], F32, bstack)
                        svb = sb("svb", [128, 4, 512], BF16, bstack)
                        ssv = sb("ssv", [128, 4], F32, bstack)
                        gsv_sb = sb("gsv_sb", [128, 512], F32, bstack)
                        wsg_f = sb("wsg_f", [128, 8, 128], F32, bstack)
                        wsg_b = sb("wsg_b", [128, 8, 128], BF16, bstack)
                        bsg_f = sb("bsg_f", [1, 1024], F32, bstack)
                        bsg_h = sb("bsg_h", [1, 1024], BF16, bstack)
                        bsg_l = sb("bsg_l", [1, 1024], BF16, bstack)
                        sgT = sb("sgT", [64, 8, 512], BF16, bstack)
                        mrg = sb("mrg", [128, 8, 512], BF16, bstack)
                        hT = mrg
                        c = cols(tg)
                        P.dma("sp", lambda h: h.dma_start(out=gsv_sb[:, :], in_=gsv[l, :, :]), writes=("gsv",), key="gsv")
                        wsg_src = wsg4T if tg == 4 else wsgT
                        bsg_src = bsg4 if tg == 4 else bsg
                        sgmask = MBLE if tg == 4 else MLE
                        P.dma("sp", lambda h, wsg_src=wsg_src: h.dma_start(out=wsg_f[:, :, :], in_=wsg_src[l, :, :, :]), writes=("wsg_f",), key="wsg_f")
                        P.dma("sp", lambda h, bsg_src=bsg_src: h.dma_start(out=bsg_f[:, :], in_=bsg_src[l, :, :]), writes=("bsg_f",), key="bsg_f")
                        for g in range(8):
                            P.op("dve", lambda h, g=g, sgmask=sgmask: h.tensor_tensor(out=wsg_b[:, g, :], in0=wsg_f[:, g, :], in1=sgmask, op=ALU.mult),
                                 reads=("wsg_f", "cst"), writes=("wsg_b",))
                        P.op("dve", lambda h: h.tensor_copy(out=bsg_h[:, :], in_=bsg_f[:, :]), reads=("bsg_f",), writes=("bsg_h",))
                        P.op("dve", lambda h: h.tensor_tensor(out=bsg_l[:, :], in0=bsg_f[:, :], in1=bsg_h[:, :], op=ALU.subtract),
                             reads=("bsg_f", "bsg_h"), writes=("bsg_l",))
                        rmsnorm(tg, gmix_sb[:, l, :])

                        def gelu_from(bankap, dst, parts):
                            pp = slice(0, parts)
                            P.op("act", lambda h: h.activation(out=t1[pp, :], in_=bankap, func=AF.Square), reads=rk, writes=("t1",))
                            P.op("dve", lambda h: h.tensor_scalar(out=t1[pp, :], in0=t1[pp, :], scalar1=GELU_C, scalar2=1.0, op0=ALU.mult, op1=ALU.add),
                                 reads=("t1",), writes=("t1",))
                            P.op("dve", lambda h: h.tensor_tensor(out=t1[pp, :], in0=t1[pp, :], in1=bankap, op=ALU.mult), reads=("t1",) + rk, writes=("t1",))
                            P.op("act", lambda h: h.activation(out=t2[pp, :], in_=t1[pp, :], func=AF.Sigmoid, scale=GELU_S), reads=("t1",), writes=("t2",))
                            P.op("dve", lambda h: h.tensor_tensor(out=dst, in0=t2[pp, :], in1=bankap, op=ALU.mult), reads=("t2",) + rk, writes=wkk)

                        wu, wu_key = load_w(win[l, :, :, 1536:2048], 128, (8, 512))
                        for g in range(8):
                            b = g % 2
                            for kc in range(8):
                                P.op("pe", lambda h, b=b, g=g, kc=kc, wu=wu: h.matmul(bank(b)[0:64, :], lhsT=wu[:, kc, g * 64:(g + 1) * 64], rhs=xn[:, kc, :],
                                                                                    start=(kc == 0), stop=(kc == 7)),
                                     reads=("xn", wu_key), writes=(bk(b),))
                            rk = (bk(b),)
                            wkk = ("uT",)
                            gelu_from(bank(b)[0:64, :], uT[:, g, :], 64)
                        wsv, wsv_key = load_w(win[l, :, :, 2048:2560], 128, (8, 512))
                        for tb in range(4):
                            b = 2 + tb % 2
                            for kc in range(8):
                                P.op("pe", lambda h, b=b, tb=tb, kc=kc, wsv=wsv: h.matmul(bank(b), lhsT=xn[:, kc, tb * 128:(tb + 1) * 128], rhs=wsv[:, kc, :],
                                                                                        start=(kc == 0), stop=(kc == 7)),
                                     reads=("xn", wsv_key), writes=(bk(b),))
                            rk = (bk(b),)
                            wkk = ("svg",)
                            gelu_from(bank(b), svg[:, :], 128)
                            P.op("dve", lambda h, tb=tb: h.memset(ssv[:, tb:tb + 1], 0.0), writes=("ssv",))
                            P.op("act", lambda h, tb=tb: h.activation(out=t1[:, :], in_=svg[:, :], func=AF.Square, accum_out=ssv[:, tb:tb + 1]),
                                 reads=("svg", "ssv"), writes=("t1", "ssv"))
                            P.op("act", lambda h, tb=tb: h.activation(out=ssv[:, tb:tb + 1], in_=ssv[:, tb:tb + 1], func=AF.Ln, bias=EPS, scale=1.0 / 512),
                                 reads=("ssv",), writes=("ssv",))
                            P.op("act", lambda h, tb=tb: h.activation(out=ssv[:, tb:tb + 1], in_=ssv[:, tb:tb + 1], func=AF.Exp, scale=-0.5),
                                 reads=("ssv",), writes=("ssv",))
                            P.op("dve", lambda h, tb=tb: h.scalar_tensor_tensor(out=svn[:, :], in0=svg[:, :], scalar=ssv[:, tb:tb + 1], in1=gsv_sb[:, :],
                                                                                 op0=ALU.mult, op1=ALU.mult),
                                 reads=("svg", "ssv", "gsv"), writes=("svn",))
                            P.op("dve", lambda h, tb=tb: h.tensor_copy(out=svb[:, tb, :], in_=svn[:, :]), reads=("svn",), writes=("svb",))
                            if tg == 4:
                                P.dma("sp", lambda h, tb=tb: h.dma_start(out=sgv[l, tb * 128:(tb + 1) * 128, :], in_=svn[:, :]), reads=("svn",), key="svn")
                        for g in range(8):
                            b = 4 + g % 2
                            for tb in range(4):
                                o = bank(b)[0:64, tb * 128:(tb + 1) * 128]
                                P.op("pe", lambda h, o=o, g=g, tb=tb: h.matmul(o, lhsT=svb[:, tb, g * 64:(g + 1) * 64], rhs=wsg_b[:, g, :], start=True, stop=False),
                                     reads=("svb", "wsg_b"), writes=(bk(b),))
                                P.op("pe", lambda h, o=o, g=g: h.matmul(o, lhsT=ones_bf[0:1, 0:64], rhs=bsg_h[0:1, g * 128:(g + 1) * 128], start=False, stop=False),
                                     reads=("bsg_h", "ones"), writes=(bk(b),))
                                P.op("pe", lambda h, o=o, g=g: h.matmul(o, lhsT=ones_bf[0:1, 0:64], rhs=bsg_l[0:1, g * 128:(g + 1) * 128], start=False, stop=True),
                                     reads=("bsg_l", "ones"), writes=(bk(b),))
                            P.op("dve", lambda h, b=b, g=g: h.tensor_tensor(out=sgT[:, g, :], in0=uT[:, g, :], in1=bank(b)[0:64, :], op=ALU.mult),
                                 reads=("uT", bk(b)), writes=("sgT",))
                        for e4 in range(2):
                            wga, wga_key = load_w(win[l, :, :, 2560 + e4 * 512: 2560 + (e4 + 1) * 512], 128, (8, 512))
                            woa, woa_key = load_w(woatt[l, :, :, e4 * 512:(e4 + 1) * 512], 64, (8, 512))
                            for ee in range(4):
                                e = e4 * 4 + ee
                                for hd in range(8):
                                    P.op("pe", lambda h, hd=hd, ee=ee, woa=woa: h.matmul(bank(0), lhsT=woa[:, hd, ee * 128:(ee + 1) * 128], rhs=attT[:, hd, :],
                                                                                       start=(hd == 0), stop=(hd == 7)),
                                         reads=("attT", woa_key), writes=(bk(0),))
                                for kc in range(8):
                                    P.op("pe", lambda h, kc=kc, ee=ee, wga=wga: h.matmul(bank(1), lhsT=wga[:, kc, ee * 128:(ee + 1) * 128], rhs=xn[:, kc, :],
                                                                                       start=(kc == 0), stop=(kc == 7)),
                                         reads=("xn", wga_key), writes=(bk(1),))
                                P.op("act", lambda h, e=e: h.activation(out=t1[:, :], in_=bank(1), func=AF.Sigmoid, bias=bgate_sb[:, l, e:e + 1]),
                                     reads=(bk(1), "params"), writes=("t1",))
                                P.op("dve", lambda h, e=e: h.tensor_tensor(out=mrg[:, e, :], in0=t1[:, :], in1=bank(0), op=ALU.mult),
                                     reads=("t1", bk(0)), writes=("mrg",))
                            wgs, wgs_key = load_w(win[l, :, :, 3584 + e4 * 512: 3584 + (e4 + 1) * 512], 128, (8, 512))
                            wos, wos_key = load_w(wosg[l, :, :, e4 * 512:(e4 + 1) * 512], 64, (8, 512))
                            for ee in range(4):
                                e = e4 * 4 + ee
                                for g in range(8):
                                    P.op("pe", lambda h, g=g, ee=ee, wos=wos: h.matmul(bank(2), lhsT=wos[:, g, ee * 128:(ee + 1) * 128], rhs=sgT[:, g, :],
                                                                                     start=(g == 0), stop=(g == 7)),
                                         reads=("sgT", wos_key), writes=(bk(2),))
                                for kc in range(8):
                                    P.op("pe", lambda h, kc=kc, ee=ee, wgs=wgs: h.matmul(bank(3), lhsT=wgs[:, kc, ee * 128:(ee + 1) * 128], rhs=xn[:, kc, :],
                                                                                       start=(kc == 0), stop=(kc == 7)),
                                         reads=("xn", wgs_key), writes=(bk(3),))
                                P.op("act", lambda h, e=e: h.activation(out=t1[:, :], in_=bank(3), func=AF.Sigmoid, bias=bgate_sb[:, l, 8 + e:9 + e]),
                                     reads=(bk(3), "params"), writes=("t1",))
                                P.op("dve", lambda h: h.tensor_tensor(out=t2[:, :], in0=t1[:, :], in1=bank(2), op=ALU.mult),
                                     reads=("t1", bk(2)), writes=("t2",))
                                P.op("dve", lambda h, e=e: h.tensor_tensor(out=mrg[:, e, :], in0=mrg[:, e, :], in1=t2[:, :], op=ALU.add),
                                     reads=("t2", "mrg"), writes=("mrg",))
                        for e4 in range(2):
                            wo_t, wo_key = load_w(wo[l, :, :, e4 * 512:(e4 + 1) * 512], 128, (8, 512))
                            for ee in range(4):
                                e = e4 * 4 + ee
                                b = ee % 2
                                for kc in range(8):
                                    P.op("pe", lambda h, b=b, kc=kc, ee=ee, wo_t=wo_t: h.matmul(bank(b), lhsT=wo_t[:, kc, ee * 128:(ee + 1) * 128], rhs=mrg[:, kc, :],
                                                                                             start=(kc == 0), stop=(kc == 7)),
                                         reads=("mrg", wo_key), writes=(bk(b),))
                                P.op("dve", lambda h, b=b, e=e: h.tensor_tensor(out=xT[:, e, c], in0=xT[:, e, c], in1=bank(b), op=ALU.add),
                                     reads=(bk(b), "xT%d" % tg), writes=("xT%d" % tg,))
                        rmsnorm(tg, gffn_sb[:, l, :])
                        for fq in range(4):
                            for f4 in range(2):
                                w1_t, w1_key = load_w(w1[l, :, :, fq * 1024 + f4 * 512: fq * 1024 + (f4 + 1) * 512], 128, (8, 512))
                                for ff in range(4):
                                    fi = f4 * 4 + ff
                                    b = ff % 2
                                    for kc in range(8):
                                        P.op("pe", lambda h, b=b, kc=kc, ff=ff, w1_t=w1_t: h.matmul(bank(b), lhsT=w1_t[:, kc, ff * 128:(ff + 1) * 128], rhs=xn[:, kc, :],
                                                                                                 start=(kc == 0), stop=(kc == 7)),
                                             reads=("xn", w1_key), writes=(bk(b),))
                                    P.op("act", lambda h, b=b: h.activation(out=t1[:, :], in_=bank(b), func=AF.Relu), reads=(bk(b),), writes=("t1",))
                                    P.op("dve", lambda h, fi=fi: h.tensor_tensor(out=hT[:, fi, :], in0=t1[:, :], in1=t1[:, :], op=ALU.mult),
                                         reads=("t1",), writes=("mrg",))
                            for e2 in range(2):
                                w2_t, w2_key = load_w(w2[l, :, fq * 8:(fq + 1) * 8, e2 * 512:(e2 + 1) * 512], 128, (8, 512))
                                for ee in range(4):
                                    e = e2 * 4 + ee
                                    b = 2 + ee % 2
                                    for fi in range(8):
                                        P.op("pe", lambda h, b=b, fi=fi, ee=ee, w2_t=w2_t: h.matmul(bank(b), lhsT=w2_t[:, fi, ee * 128:(ee + 1) * 128], rhs=hT[:, fi, :],
                                                                                                 start=(fi == 0), stop=(fi == 7)),
                                             reads=("mrg", w2_key), writes=(bk(b),))
                                    P.op("dve", lambda h, b=b, e=e: h.tensor_tensor(out=xT[:, e, c], in0=xT[:, e, c], in1=bank(b), op=ALU.add),
                                         reads=(bk(b), "xT%d" % tg), writes=("xT%d" % tg,))
                        P.barrier()
                        P.flush()
                    chk("D%d_%d" % (l, tg))

        fstack = contextlib.ExitStack()
        with fstack:
            yst = sb("yst", [128, 8, 512], F32, fstack)
            for tg in range(NTG):
                c = cols(tg)
                P.op("act", lambda h, c=c: h.activation(out=sq[:, :, :], in_=xT[:, :, c], func=AF.Square), reads=("xT%d" % tg,), writes=("xn",))
                for kc in range(8):
                    P.op("pe", lambda h, kc=kc: h.matmul(bank(7), lhsT=ones_bf[:, :], rhs=sq[:, kc, :], start=(kc == 0), stop=(kc == 7)),
                         reads=("xn", "ones"), writes=(bk(7),))
                P.op("act", lambda h: h.activation(out=lnv[:, :], in_=bank(7), func=AF.Ln, bias=EPS, scale=1.0 / D), reads=(bk(7),), writes=("rstd",))
                P.op("act", lambda h: h.activation(out=rstd[:, :], in_=lnv[:, :], func=AF.Exp, scale=-0.5), reads=("rstd",), writes=("rstd",))
                for kc in range(8):
                    P.op("dve", lambda h, kc=kc, c=c: h.scalar_tensor_tensor(out=yst[:, kc, :], in0=xT[:, kc, c], scalar=gfin_sb[:, kc:kc + 1], in1=rstd[:, :],
                                                                              op0=ALU.mult, op1=ALU.mult),
                         reads=("xT%d" % tg, "rstd", "params"), writes=("yst",))
                if tg < 4:
                    P.dma("sp", lambda h, c=c: h.dma_start(out=yp[:, c].rearrange("(kc kp) t -> kp kc t", kp=128), in_=yst[:, :, :]), reads=("yst",), key="yst")
                else:
                    P.dma("sp", lambda h: h.dma_start(out=ys[:, :].rearrange("(kc kp) t -> kp kc t", kp=128), in_=yst[:, :, :]), reads=("yst",), key="yst")
            P.barrier()
            P.flush()
    except _Stop:
        pass
    return nc


_NC_CACHE = {}


def _consts():
    s = np.arange(128)[:, None]
    t = np.arange(128)[None, :]
    negtri = np.where(s >= t, -1.0, 0.0)
    negtric = np.where(s < t, -1.0, 0.0)
    mgt = np.where(t > s, 1.0, 0.0)
    mle = np.where(s <= t, 1.0, 0.0)
    mb = np.zeros((128, 128))
    for a in range(64):
        for b in range(64):
            if a // 4 == b // 4 and (a % 4) < (b % 4):
                mb[a, b] = 1.0
    mble = np.where((s // 4 == t // 4) & (s % 4 <= t % 4), 1.0, 0.0)
    return np.concatenate([negtri, negtric, mgt, mle, mb, mble], axis=1).astype(np.float32)


def _masks(j):
    s = np.arange(128)[:, None]
    t = np.arange(128)[None, :]
    tri = (t > s).astype(np.float32)
    out = np.zeros((64, 128, 512), np.float32)
    gs = groups_of(j)
    for slot in range(4):
        G = gs[slot]
        for r in range(16):
            kb = 16 * slot + r
            for qb in range(4):
                qblk = 4 * G + qb
                if kb < qblk:
                    out[slot * 16 + r, :, qb * 128:(qb + 1) * 128] = 1.0
                elif kb == qblk:
                    out[slot * 16 + r, :, qb * 128:(qb + 1) * 128] = tri
    return out.astype(ml_dtypes.bfloat16)


def kernel(x_prompt, x_sample, cache_k, cache_v, page_table, g_mix, w_in, b_gate, b_sb, g_sv, w_sg, b_sg,
           w_o_att, w_o_sg, w_o, g_ffn, w_ff1, w_ff2, g_final):
    f = np.float32
    A = lambda a: np.ascontiguousarray(np.asarray(a))
    x_prompt = np.asarray(x_prompt, f); x_sample = np.asarray(x_sample, f)
    cache_k = np.asarray(cache_k); cache_v = np.asarray(cache_v)
    w_in = np.asarray(w_in, f)
    if "nc" not in _NC_CACHE:
        _NC_CACHE["nc"] = build_nc()
    nc = _NC_CACHE["nc"]

    def kmaj(w, kp=128):
        L, K, E = w.shape
        return A(w.reshape(L, K // kp, kp, E).transpose(0, 2, 1, 3))

    def vec(g):
        L, N = g.shape
        return A(g.reshape(L, N // 128, 128).transpose(0, 2, 1))

    shared = {
        "xs": A(x_sample.reshape(512, D).T),
        "win": kmaj(w_in),
        "woatt": A(np.asarray(w_o_att, f).reshape(2, 8, 64, D).transpose(0, 2, 1, 3)),
        "wosg": A(np.asarray(w_o_sg, f).reshape(2, 8, 64, D).transpose(0, 2, 1, 3)),
        "wo": kmaj(np.asarray(w_o, f)),
        "w1": kmaj(np.asarray(w_ff1, f)),
        "w2": kmaj(np.asarray(w_ff2, f)),
        "gmix": vec(np.asarray(g_mix, f)),
        "gffn": vec(np.asarray(g_ffn, f)),
        "gfin": A(np.asarray(g_final, f).reshape(8, 128).T),
        "bgate": vec(np.asarray(b_gate, f)),
        "bsb": A(np.broadcast_to(np.asarray(b_sb, f)[:, None, :], (2, 128, 8))),
        "gsv": A(np.broadcast_to(np.asarray(g_sv, f)[:, None, :], (2, 128, 512))),
        "wsgT": A(np.asarray(w_sg, f).transpose(0, 3, 1, 2)),
        "bsg": A(np.asarray(b_sg, f).reshape(2, 1, 1024)),
        "wsg4T": A(np.tile(np.asarray(w_sg, f)[:, :, :4, :4].transpose(0, 3, 1, 2), (1, 32, 1, 32))),
        "bsg4": A(np.tile(np.asarray(b_sg, f)[:, :, :4], (1, 1, 32)).reshape(2, 1, 1024)),
        "cst": _consts(),
        "ptab": A(np.asarray(page_table, np.int32).reshape(16, 128, 1)),
    }
    in_maps = []
    for c in range(8):
        b, j = c // 4, c % 4
        gs = groups_of(j)
        xpc = np.concatenate([x_prompt[b, G * 512:(G + 1) * 512, :] for G in gs], axis=0)
        m = dict(shared)
        m["xp"] = A(xpc.T)
        m["wsamp"] = A(np.concatenate([m["win"][:, :, :, c * 64:(c + 1) * 64], m["win"][:, :, :, 512 + c * 64:512 + (c + 1) * 64],
                                       m["win"][:, :, :, 1024 + c * 64:1024 + (c + 1) * 64]], axis=3))
        m["bsbc"] = A(np.broadcast_to(np.asarray(b_sb, f)[:, None, c:c + 1], (2, 128, 1)))
        m["masks"] = _masks(j)
        for l in range(2):
            m["poolK%d" % l] = A(cache_k[l, :, :, c, :].transpose(0, 2, 1)).reshape(NPOOL, 8192)
            m["poolV%d" % l] = A(cache_v[l, :, :, c, :]).reshape(NPOOL, 8192)
        in_maps.append(m)
    res = run_bass_kernel_spmd(nc, in_maps, core_ids=list(range(8))).results

    y_prompt = np.zeros((2, 8192, D), f)
    nk_p = np.zeros((2, 2, 8192, 8, 64), f)
    nv_p = np.zeros((2, 2, 8192, 8, 64), f)
    nk_s = np.zeros((2, 128, 4, 8, 64), f)
    nv_s = np.zeros((2, 128, 4, 8, 64), f)
    for c in range(8):
        b, j = c // 4, c % 4
        r = res[c]
        for slot, G in enumerate(groups_of(j)):
            sl = slice(slot * 512, (slot + 1) * 512)
            gl = slice(G * 512, (G + 1) * 512)
            y_prompt[b, gl, :] = r["yp"][:, sl].T
            for l in range(2):
                nk_p[l, b, gl] = r["nkp"][l][:, sl].T.reshape(512, 8, 64)
                nv_p[l, b, gl] = r["nvp"][l][sl, :].reshape(512, 8, 64)
        for l in range(2):
            nk_s[l, :, :, c, :] = r["nks"][l].T.reshape(128, 4, 64)
            nv_s[l, :, :, c, :] = r["nvs"][l].reshape(128, 4, 64)
    y_sample = np.ascontiguousarray(res[0]["ys"].T).reshape(128, 4, D)
    sgv_o = np.ascontiguousarray(res[0]["sgv"]).reshape(2, 128, 4, 8, 64)
    return (y_prompt, y_sample, nk_p, nv_p, nk_s, nv_s, sgv_o)
```
